# Optimizing a Trainium2 kernel written in Bass

```python
import jax, jax.numpy as jnp
from jax import lax
import numpy as np

D_MODEL = 1024
BATCH = 2
SEQ = 8192
DEPTH = 2

N_META = 16
FOX_HEADS = 16
FOX_HEAD_DIM = D_MODEL // FOX_HEADS
Q_BLOCK = 128
HGRN_EXPAND = 128
HGRN_HEADS = D_MODEL // HGRN_EXPAND
HGRN_CHUNK = 64
D_FF = ((8 * D_MODEL // 3 + 127) // 128) * 128
CONV_WIDTH = 3
EPS = 1e-6
N_FOX_LAYERS = (DEPTH + 1) // 2
N_HGRN_LAYERS = DEPTH // 2

kernel_name = "fox_hgrn2_interleaved_convffn"


def rms_norm(x, gain):
    x32 = x.astype(jnp.float32)
    y = x32 * lax.rsqrt(jnp.mean(x32 * x32, axis=-1, keepdims=True) + EPS)
    return (y * gain.astype(jnp.float32)).astype(x.dtype)


def fox_attention(h, norm_g, w_in, b_f, q_gain, k_gain, w_out):
    B, L, D = h.shape
    H, hd = FOX_HEADS, FOX_HEAD_DIM
    xn = rms_norm(h, norm_g)
    q, k, v, f_logit, o_gate = jnp.split(xn @ w_in, [D, 2 * D, 3 * D, 3 * D + H], axis=-1)
    q = rms_norm(q.reshape(B, L, H, hd), q_gain).astype(jnp.float32) * (hd ** -0.5)
    k = rms_norm(k.reshape(B, L, H, hd), k_gain).astype(jnp.float32)
    v = v.reshape(B, L, H, hd)
    log_f = jax.nn.log_sigmoid((f_logit + b_f).astype(jnp.float32))
    c = jnp.cumsum(log_f, axis=1).transpose(0, 2, 1)
    key_pos = jnp.arange(L)

    def attend(q_blk, c_blk, q_pos):
        s = jnp.einsum('bqhd,bkhd->bhqk', q_blk, k)
        s = s + c_blk[..., None] - c[:, :, None, :]
        s = jnp.where(q_pos[:, None] >= key_pos[None, :], s, -jnp.inf)
        p = jax.nn.softmax(s, axis=-1)
        return jnp.einsum('bhqk,bkhd->bqhd', p.astype(v.dtype), v)

    o_meta = attend(q[:, :N_META], c[:, :, :N_META], jnp.arange(N_META))
    n_blk = (L - N_META) // Q_BLOCK
    q_r = q[:, N_META:].reshape(B, n_blk, Q_BLOCK, H, hd).transpose(1, 0, 2, 3, 4)
    c_r = c[:, :, N_META:].reshape(B, H, n_blk, Q_BLOCK).transpose(2, 0, 1, 3)
    pos_r = (N_META + jnp.arange(L - N_META)).reshape(n_blk, Q_BLOCK)
    o_r = lax.map(lambda a: attend(*a), (q_r, c_r, pos_r))
    o_r = o_r.transpose(1, 0, 2, 3, 4).reshape(B, L - N_META, H, hd)
    o = jnp.concatenate([o_meta, o_r], axis=1).reshape(B, L, D)
    o = o * jax.nn.sigmoid(o_gate)
    return o @ w_out


def hgrn2_chunk(S, q, k, v, log_f):
    C = q.shape[2]
    b = jnp.cumsum(log_f, axis=2)
    causal = jnp.arange(C)[:, None] >= jnp.arange(C)[None, :]
    diff = b[:, :, :, None, :] - b[:, :, None, :, :]
    decay = jnp.exp(jnp.where(causal[None, None, :, :, None], diff, -jnp.inf))
    scores = jnp.einsum('bhtk,bhtsk,bhsk->bhts', q, decay, k)
    o = jnp.einsum('bhts,bhsv->bhtv', scores, v) + jnp.einsum('bhtk,bhkv->bhtv', q * jnp.exp(b), S)
    b_end = b[:, :, -1]
    S_new = jnp.exp(b_end)[..., None] * S + jnp.einsum(
        'bhsk,bhsv->bhkv', k * jnp.exp(b_end[:, :, None, :] - b), v)
    return S_new, o


def hgrn2_mixer(h, norm_g, w_in, lower_bound, o_gain, w_out):
    B, L, D = h.shape
    H, dk = HGRN_HEADS, HGRN_EXPAND
    xn = rms_norm(h, norm_g)
    q, f_logit, i, g = jnp.split(xn @ w_in, 4, axis=-1)
    lb = lower_bound.astype(jnp.float32)
    f = lb + (1.0 - lb) * jax.nn.sigmoid(f_logit.astype(jnp.float32))

    def to_heads(t):
        return t.astype(jnp.float32).reshape(B, L, H, dk).transpose(0, 2, 1, 3)

    q = to_heads(jax.nn.silu(q))
    k = to_heads(1.0 - f)
    log_f = to_heads(jnp.log(f))
    v = to_heads(i)
    S0 = jnp.zeros((B, H, dk, dk), jnp.float32)
    S, o_meta = hgrn2_chunk(S0, q[:, :, :N_META], k[:, :, :N_META], v[:, :, :N_META], log_f[:, :, :N_META])
    n_chunk = (L - N_META) // HGRN_CHUNK

    def chunks(t):
        return t[:, :, N_META:].reshape(B, H, n_chunk, HGRN_CHUNK, dk).transpose(2, 0, 1, 3, 4)

    _, o_r = lax.scan(lambda s, a: hgrn2_chunk(s, *a), S, (chunks(q), chunks(k), chunks(v), chunks(log_f)))
    o_r = o_r.transpose(1, 2, 0, 3, 4).reshape(B, H, L - N_META, dk)
    o = jnp.concatenate([o_meta, o_r], axis=2).transpose(0, 2, 1, 3)
    o = rms_norm(o, o_gain).reshape(B, L, D) * jax.nn.sigmoid(g.astype(jnp.float32))
    return o.astype(h.dtype) @ w_out


def conv_ffn(h, norm_g, w_gate, w_up, conv_w, conv_b, w_down):
    xn = rms_norm(h, norm_g)
    a = xn @ w_gate
    L = a.shape[1]
    a_pad = jnp.pad(a, ((0, 0), (CONV_WIDTH - 1, 0), (0, 0)))
    a = (conv_b + conv_w[0] * a_pad[:, 0:L] + conv_w[1] * a_pad[:, 1:L + 1]
         + conv_w[2] * a_pad[:, 2:L + 2])
    return (jax.nn.silu(a) * (xn @ w_up)) @ w_down


def setup_inputs(seed: int = 0) -> dict:
    key = jax.random.key(seed)
    ks = jax.random.split(key, 19)
    D, F, H = D_MODEL, D_FF, FOX_HEADS
    nF, nH = N_FOX_LAYERS, N_HGRN_LAYERS

    def normal(k, shape, scale):
        return scale * jax.random.normal(k, shape, jnp.float32)

    def gain(k, shape):
        return 1.0 + normal(k, shape, 0.05)

    return {
        "x": normal(ks[0], (BATCH, SEQ, D), 1.0),
        "meta_tokens": normal(ks[1], (N_META, D), 1.0),
        "fox_norm": gain(ks[2], (nF, D)),
        "fox_w_in": normal(ks[3], (nF, D, 4 * D + H), D ** -0.5),
        "fox_b_f": 2.0 + normal(ks[4], (nF, H), 0.1),
        "fox_q_gain": gain(ks[5], (nF, FOX_HEAD_DIM)),
        "fox_k_gain": gain(ks[6], (nF, FOX_HEAD_DIM)),
        "fox_w_out": normal(ks[7], (nF, D, D), D ** -0.5),
        "hgrn_norm": gain(ks[8], (nH, D)),
        "hgrn_w_in": normal(ks[9], (nH, D, 4 * D), D ** -0.5),
        "hgrn_lower_bounds": normal(ks[10], (DEPTH, D), 0.1),
        "hgrn_o_gain": gain(ks[11], (nH, HGRN_EXPAND)),
        "hgrn_w_out": normal(ks[12], (nH, D, D), D ** -0.5),
        "ffn_norm": gain(ks[13], (DEPTH, D)),
        "ffn_w_gate": normal(ks[14], (DEPTH, D, F), D ** -0.5),
        "ffn_w_up": normal(ks[15], (DEPTH, D, F), D ** -0.5),
        "ffn_conv_w": normal(ks[16], (DEPTH, CONV_WIDTH, F), CONV_WIDTH ** -0.5),
        "ffn_conv_b": normal(ks[17], (DEPTH, F), 0.02),
        "ffn_w_down": normal(ks[18], (DEPTH, F, D), F ** -0.5),
    }


def reference(x, meta_tokens, fox_norm, fox_w_in, fox_b_f, fox_q_gain, fox_k_gain, fox_w_out,
              hgrn_norm, hgrn_w_in, hgrn_lower_bounds, hgrn_o_gain, hgrn_w_out,
              ffn_norm, ffn_w_gate, ffn_w_up, ffn_conv_w, ffn_conv_b, ffn_w_down):
    B = x.shape[0]
    meta = jnp.broadcast_to(meta_tokens[None].astype(x.dtype), (B, N_META, D_MODEL))
    h = jnp.concatenate([meta, x], axis=1)
    p_lb = jax.nn.softmax(hgrn_lower_bounds.astype(jnp.float32), axis=0)
    lower_bounds = jnp.cumsum(p_lb, axis=0) - p_lb[0]
    for layer in range(DEPTH):
        j = layer // 2
        if layer % 2 == 0:
            h = h + fox_attention(h, fox_norm[j], fox_w_in[j], fox_b_f[j], fox_q_gain[j],
                                  fox_k_gain[j], fox_w_out[j])
        else:
            h = h + hgrn2_mixer(h, hgrn_norm[j], hgrn_w_in[j], lower_bounds[layer],
                                hgrn_o_gain[j], hgrn_w_out[j])
        h = h + conv_ffn(h, ffn_norm[layer], ffn_w_gate[layer], ffn_w_up[layer],
                         ffn_conv_w[layer], ffn_conv_b[layer], ffn_w_down[layer])
    return h[:, N_META:]
```

```python
import contextlib
import numpy as np
import ml_dtypes
import concourse.bass as bass
import concourse.mybir as mybir
from concourse.bass_utils import run_bass_kernel_spmd

F32 = mybir.dt.float32
BF16 = mybir.dt.bfloat16
AF = mybir.ActivationFunctionType
ALU = mybir.AluOpType
AX = mybir.AxisListType

NCORES = 8
D = 1024
NMETA = 16
SEQ = 8192
L = SEQ + NMETA
B = 2
TT = 114
NT_B = L // TT
CH = L // 4
NT_C = CH // TT
FF = 2816
NFC = FF // 128
EPS = 1e-6
H_FOX = 16
HD = 64
H_HG = 8
DK = 128
N_DMA_SEMS = 24


class Prog:
    ENG = ("pe", "act", "dve", "pool", "sp")

    def __init__(self, nc, stack):
        self.nc = nc
        self.stack = stack
        self.ops = {e: [] for e in self.ENG}
        self.last_w = {}
        self.readers = {}
        self.n_dma = 0
        self.dma_last = {}
        self.dma_uses = {}
        self.out_dmas = []
        self.uid = 0

    def sb(self, name, shape, dtype):
        return self.stack.enter_context(self.nc.sbuf_tensor(name, list(shape), dtype))

    def ps(self, name, dtype=F32):
        n = 512 if dtype == F32 else 1024
        return self.stack.enter_context(self.nc.psum_tensor(name, [128, n], dtype))

    def op(self, eng, fn, reads=(), writes=(), dma=False, is_out=False):
        lst = self.ops[eng]
        idx = len(lst)
        deps = set()
        for r in reads:
            if r in self.last_w:
                deps.add(self.last_w[r])
        for w in writes:
            if w in self.last_w:
                deps.add(self.last_w[w])
            for e2, i2 in self.readers.get(w, {}).items():
                deps.add((e2, i2))
        rec = dict(fn=fn, deps=deps, dma=dma, signaled=False, sem=None, target=None)
        if dma:
            s = self.n_dma % N_DMA_SEMS
            self.n_dma += 1
            if s in self.dma_last:
                deps.add(self.dma_last[s])
            self.dma_last[s] = (eng, idx)
            self.dma_uses[s] = self.dma_uses.get(s, 0) + 1
            rec["sem"] = s
            rec["target"] = 16 * self.dma_uses[s]
            rec["signaled"] = True
        deps2 = set()
        for (e2, i2) in deps:
            if e2 == eng and i2 == idx:
                continue
            if e2 == eng and eng == "pe" and not self.ops[e2][i2]["dma"]:
                continue
            deps2.add((e2, i2))
        rec["deps"] = deps2
        lst.append(rec)
        for w in writes:
            self.last_w[w] = (eng, idx)
            self.readers[w] = {}
        for r in reads:
            self.readers.setdefault(r, {})[eng] = idx
        if is_out:
            self.out_dmas.append((eng, idx))
        return (eng, idx)

    def emit(self):
        nc = self.nc
        st = self.stack
        for e in self.ENG:
            for rec in self.ops[e]:
                for (e2, i2) in rec["deps"]:
                    self.ops[e2][i2]["signaled"] = True
        for e in self.ENG:
            c = 0
            for rec in self.ops[e]:
                if rec["signaled"] and not rec["dma"]:
                    c += 1
                    rec["sig"] = c
        esem = {e: st.enter_context(nc.semaphore("sem_" + e)) for e in self.ENG}
        dsem = [st.enter_context(nc.semaphore("dsem%d" % i)) for i in range(N_DMA_SEMS)]
        block = st.enter_context(nc.Block())
        hw = {"pe": block.tensor, "act": block.scalar, "dve": block.vector,
              "pool": block.gpsimd, "sp": block.sync}
        ops = self.ops
        out_dmas = self.out_dmas

        def make(e):
            def body(eng):
                waited_e = {}
                waited_d = {}
                for rec in ops[e]:
                    for (e2, i2) in sorted(rec["deps"]):
                        src = ops[e2][i2]
                        if src["dma"]:
                            s, t = src["sem"], src["target"]
                            if waited_d.get(s, 0) < t:
                                eng.wait_ge(dsem[s], t)
                                waited_d[s] = t
                        else:
                            k = src["sig"]
                            if waited_e.get(e2, 0) < k:
                                eng.wait_ge(esem[e2], k)
                                waited_e[e2] = k
                    ins = rec["fn"](eng)
                    if rec["dma"]:
                        ins.then_inc(dsem[rec["sem"]], 16)
                    elif rec["signaled"]:
                        ins.then_inc(esem[e], 1)
                if e == "pool":
                    for (e2, i2) in out_dmas:
                        src = ops[e2][i2]
                        eng.wait_ge(dsem[src["sem"]], src["target"])
            return body

        for e in self.ENG:
            if ops[e] or e == "pool":
                hw[e](make(e))


def new_prog():
    nc = bass.Bass("TRN2", target_bir_lowering=False)
    stack = contextlib.ExitStack()
    return nc, stack, Prog(nc, stack)


def dram_in(nc, name, shape, dt=F32):
    return nc.dram_tensor(name, list(shape), dt, kind="ExternalInput").ap()


def dram_out(nc, name, shape, dt=F32):
    return nc.dram_tensor(name, list(shape), dt, kind="ExternalOutput").ap()


def load_weight_bf16(P, w_dram, k_rows, n_cols, name, col_block=2048):
    nk = k_rows // 128
    wt = P.sb(name, [128, nk, n_cols], BF16)
    if not hasattr(P, "wstg"):
        P.wstg = [P.sb("wstg%d" % i, [128, 2056], F32) for i in range(2)]
        P.wstg_j = 0
    col_block = min(col_block, 2056)
    for kc in range(nk):
        for c0 in range(0, n_cols, col_block):
            cw = min(col_block, n_cols - c0)
            j = P.wstg_j
            P.wstg_j += 1
            s = P.wstg[j % 2]
            sk = "wstg%d" % (j % 2)
            P.op("sp", lambda e, s=s, kc=kc, c0=c0, cw=cw: e.dma_start(
                out=s[:, 0:cw], in_=w_dram[kc * 128:(kc + 1) * 128, c0:c0 + cw]),
                writes=[sk], dma=True)
            ce = "pool" if j % 2 == 0 else "dve"
            P.op(ce, lambda e, s=s, kc=kc, c0=c0, cw=cw: e.tensor_copy(
                out=wt[:, kc, c0:c0 + cw], in_=s[:, 0:cw]),
                reads=[sk], writes=[(name, kc, c0)])
    P.wkeys = getattr(P, "wkeys", {})
    P.wkeys[name] = [(name, kc, c0) for kc in range(nk) for c0 in range(0, n_cols, col_block)]
    return wt


def make_identity(P, name="ident"):
    idt = P.sb(name, [128, 128], BF16)
    idf = P.sb(name + "_f", [128, 128], F32)
    P.op("pool", lambda e: e.memset(idf[:], 1.0), writes=[name + "_f"])
    P.op("pool", lambda e: e.affine_select(out=idf[:], in_=idf[:], pattern=[[-1, 128]],
                                           compare_op=ALU.is_ge, fill=0.0, base=0,
                                           channel_multiplier=1),
         reads=[name + "_f"], writes=[name + "_f"])
    P.op("pool", lambda e: e.affine_select(out=idf[:], in_=idf[:], pattern=[[1, 128]],
                                           compare_op=ALU.is_ge, fill=0.0, base=0,
                                           channel_multiplier=-1),
         reads=[name + "_f"], writes=[name + "_f"])
    P.op("pool", lambda e: e.tensor_copy(out=idt[:], in_=idf[:]), reads=[name + "_f"], writes=[name])
    return idt


def bcast_load(P, dram_row, n, name, rows=128):
    t = P.sb(name, [rows, n], F32)
    P.op("sp", lambda e: e.dma_start(out=t[:], in_=dram_row.partition_broadcast(rows)),
         writes=[name], dma=True)
    return t


def rms_rstd(P, ss, rstd, key_ss, key_rstd, rows, inv_n, ncol=1):
    P.op("act", lambda e: e.activation(out=rstd[0:rows, 0:ncol], in_=ss[0:rows, 0:ncol], func=AF.Sqrt,
                                       scale=inv_n, bias=EPS),
         reads=[key_ss], writes=[key_rstd])
    P.op("dve", lambda e: e.reciprocal(out=rstd[0:rows, 0:ncol], in_=rstd[0:rows, 0:ncol]),
         reads=[key_rstd], writes=[key_rstd])


def build_proj(kind):
    nc, stack, P = new_prog()
    ncols = 4112 if kind == "fox" else 4096
    x = dram_in(nc, "x", [CH, D])
    gnorm = dram_in(nc, "gnorm", [1, D])
    w_in = dram_in(nc, "w_in", [D, ncols])
    if kind == "fox":
        b_f = dram_in(nc, "b_f", [1, 16])
        q_gain = dram_in(nc, "q_gain", [1, HD])
        k_gain = dram_in(nc, "k_gain", [1, HD])
        o_q = dram_out(nc, "o_q", [CH, D], BF16)
        o_k = dram_out(nc, "o_k", [CH, D], BF16)
        o_v = dram_out(nc, "o_v", [CH, D], BF16)
        o_lf = dram_out(nc, "o_lf", [CH, 16])
        o_sg = dram_out(nc, "o_sg", [CH, D])
    else:
        lbs = dram_in(nc, "lbs", [2, D])
        o_q = dram_out(nc, "o_q", [CH, D], BF16)
        o_k = dram_out(nc, "o_k", [CH, D], BF16)
        o_v = dram_out(nc, "o_v", [CH, D], BF16)
        o_lf = dram_out(nc, "o_lf", [CH, D])
        o_sg = dram_out(nc, "o_sg", [CH, D])

    with stack:
        ident = make_identity(P)
        g_bc = bcast_load(P, gnorm[0, :], D, "g_bc", TT)
        wbf = load_weight_bf16(P, w_in, D, ncols, "wbf", col_block=2056 if kind == "fox" else 2048)
        if kind == "fox":
            bf_bc = bcast_load(P, b_f[0, :], 16, "bf_bc", TT)
            qg_bc = bcast_load(P, q_gain[0, :], HD, "qg_bc", TT)
            kg_bc = bcast_load(P, k_gain[0, :], HD, "kg_bc", TT)
            P.op("act", lambda e: e.mul(out=qg_bc[:], in_=qg_bc[:], mul=HD ** -0.5),
                 reads=["qg_bc"], writes=["qg_bc"])
        else:
            l0 = bcast_load(P, lbs[0, :], D, "l0_bc", TT)
            l1 = bcast_load(P, lbs[1, :], D, "l1_bc", TT)
            lb = P.sb("lb", [TT, D], F32)
            oml = P.sb("oml", [TT, D], F32)
            P.op("dve", lambda e: e.tensor_tensor(out=lb[:], in0=l1[:], in1=l0[:], op=ALU.subtract),
                 reads=["l0_bc", "l1_bc"], writes=["lb"])
            P.op("act", lambda e: e.activation(out=lb[:], in_=lb[:], func=AF.Sigmoid),
                 reads=["lb"], writes=["lb"])
            P.op("dve", lambda e: e.tensor_scalar(out=oml[:], in0=lb[:], scalar1=-1.0, scalar2=1.0,
                                                  op0=ALU.mult, op1=ALU.add),
                 reads=["lb"], writes=["oml"])

        NB = 2
        xt = [P.sb("xt%d" % i, [TT, D], F32) for i in range(NB)]
        junk = P.sb("junk", [TT, D], F32)
        ss = P.sb("ss", [TT, 1], F32)
        rstd = P.sb("rstd", [TT, 1], F32)
        xn = P.sb("xn", [TT, D], BF16)
        xnT = P.sb("xnT", [128, 8, TT], BF16)
        pT = P.ps("pT", BF16)
        pA = [P.ps("pA%d" % i) for i in range(2)]
        pB = [P.ps("pB%d" % i) for i in range(2)]
        pC = P.ps("pC")
        qf = P.sb("qf", [TT, D], F32)
        sq = P.sb("sq", [TT, D], F32)
        ssh = P.sb("ssh", [TT, 16], F32)
        rsh = P.sb("rsh", [TT, 16], F32)
        ob = [P.sb("ob%d" % i, [TT, D], BF16) for i in range(3)]
        of = [P.sb("of%d" % i, [TT, D], F32) for i in range(2)]
        lft = P.sb("lft", [TT, 16], F32)

        def proj_block(ps_tile, c0, cw, pkey):
            for dc in range(8):
                P.op("pe", lambda e, dc=dc: e.matmul(ps_tile[0:TT, 0:cw], lhsT=xnT[:, dc, :],
                                                      rhs=wbf[:, dc, c0:c0 + cw],
                                                      start=(dc == 0), stop=(dc == 7)),
                     reads=["xnT"] + P.wkeys["wbf"], writes=[pkey])

        def store(src, key, dst, r0, cols):
            P.op("pool", lambda e: e.dma_start(out=dst[r0:r0 + TT, 0:cols], in_=src[0:TT, 0:cols]),
                 reads=[key], dma=True, is_out=True)

        for t in range(NT_C):
            r0 = t * TT
            xb = xt[t % NB]
            xk = "xt%d" % (t % NB)
            P.op("sp", lambda e, xb=xb, r0=r0: e.dma_start(out=xb[:], in_=x[r0:r0 + TT, :]),
                 writes=[xk], dma=True)
            P.op("act", lambda e, xb=xb: e.activation(out=junk[:], in_=xb[:], func=AF.Square,
                                                       accum_out=ss[:, 0:1]),
                 reads=[xk], writes=["junk", "ss"])
            rms_rstd(P, ss, rstd, "ss", "rstd", TT, 1.0 / D)
            P.op("dve", lambda e, xb=xb: e.scalar_tensor_tensor(out=xn[:], in0=xb[:], scalar=rstd[:, 0:1],
                                                                 in1=g_bc[:], op0=ALU.mult, op1=ALU.mult),
                 reads=[xk, "rstd", "g_bc"], writes=["xn"])
            for dc in range(8):
                P.op("pe", lambda e, dc=dc: e.transpose(out=pT[:, dc * TT:(dc + 1) * TT],
                                                         in_=xn[:, dc * 128:(dc + 1) * 128],
                                                         identity=ident[0:TT, 0:TT]),
                     reads=["xn", "ident"], writes=["pT"])
            P.op("act", lambda e: e.copy(out=xnT[:].rearrange("p a b -> p (a b)"), in_=pT[:, 0:8 * TT]),
                 reads=["pT"], writes=["xnT"])

            if kind == "fox":
                for which, c0, gain, okey_i, dst in (("q", 0, qg_bc, 0, o_q), ("k", 1024, kg_bc, 1, o_k)):
                    pp = pA if which == "q" else pB
                    pk = ["pA0", "pA1"] if which == "q" else ["pB0", "pB1"]
                    for hf in range(2):
                        proj_block(pp[hf], c0 + hf * 512, 512, pk[hf])
                        P.op("act", lambda e, hf=hf, pp=pp: e.copy(out=qf[:, hf * 512:(hf + 1) * 512],
                                                                    in_=pp[hf][0:TT, :]),
                             reads=[pk[hf]], writes=["qf"])
                    P.op("dve", lambda e: e.tensor_tensor(out=sq[:], in0=qf[:], in1=qf[:], op=ALU.mult),
                         reads=["qf"], writes=["sq"])
                    P.op("dve", lambda e: e.tensor_reduce(out=ssh[:], in_=sq[:].rearrange("p (h d) -> p h d", d=HD),
                                                          axis=AX.X, op=ALU.add),
                         reads=["sq"], writes=["ssh"])
                    rms_rstd(P, ssh, rsh, "ssh", "rsh", TT, 1.0 / HD, ncol=16)
                    P.op("dve", lambda e: e.tensor_tensor(
                        out=sq[:].rearrange("p (h d) -> p h d", d=HD),
                        in0=qf[:].rearrange("p (h d) -> p h d", d=HD),
                        in1=rsh[:].to_broadcast([TT, 16, HD]) if False else rsh[:].unsqueeze(2).to_broadcast([TT, 16, HD]),
                        op=ALU.mult), reads=["qf", "rsh"], writes=["sq"])
                    o_t = ob[okey_i]
                    ok = "ob%d" % okey_i
                    P.op("dve", lambda e, o_t=o_t, gain=gain: e.tensor_tensor(
                        out=o_t[:].rearrange("p (h d) -> p h d", d=HD),
                        in0=sq[:].rearrange("p (h d) -> p h d", d=HD),
                        in1=gain[:].unsqueeze(1).to_broadcast([TT, 16, HD]),
                        op=ALU.mult), reads=["sq", "qg_bc", "kg_bc"], writes=[ok])
                    store(o_t, ok, dst, r0, D)
                for hf in range(2):
                    proj_block(pA[hf], 2048 + hf * 512, 512, "pA%d" % hf)
                    P.op("act", lambda e, hf=hf: e.copy(out=ob[2][:, hf * 512:(hf + 1) * 512], in_=pA[hf][0:TT, :]),
                         reads=["pA%d" % hf], writes=["ob2"])
                store(ob[2], "ob2", o_v, r0, D)
                proj_block(pC, 3072, 16, "pC")
                P.op("dve", lambda e: e.tensor_tensor(out=lft[:], in0=pC[0:TT, 0:16], in1=bf_bc[:], op=ALU.add),
                     reads=["pC", "bf_bc"], writes=["lft"])
                P.op("act", lambda e: e.activation(out=lft[:], in_=lft[:], func=AF.Exp, scale=-1.0),
                     reads=["lft"], writes=["lft"])
                P.op("act", lambda e: e.activation(out=lft[:], in_=lft[:], func=AF.Ln, bias=1.0),
                     reads=["lft"], writes=["lft"])
                P.op("act", lambda e: e.mul(out=lft[:], in_=lft[:], mul=-1.0),
                     reads=["lft"], writes=["lft"])
                P.op("pool", lambda e, r0=r0: e.dma_start(out=o_lf[r0:r0 + TT, :], in_=lft[:]),
                     reads=["lft"], dma=True, is_out=True)
                for hf in range(2):
                    proj_block(pB[hf], 3088 + hf * 512, 512, "pB%d" % hf)
                    P.op("act", lambda e, hf=hf: e.activation(out=of[0][:, hf * 512:(hf + 1) * 512],
                                                               in_=pB[hf][0:TT, :], func=AF.Sigmoid),
                         reads=["pB%d" % hf], writes=["of0"])
                store(of[0], "of0", o_sg, r0, D)
            else:
                for hf in range(2):
                    proj_block(pA[hf], hf * 512, 512, "pA%d" % hf)
                    P.op("act", lambda e, hf=hf: e.activation(out=ob[0][:, hf * 512:(hf + 1) * 512],
                                                               in_=pA[hf][0:TT, :], func=AF.Silu),
                         reads=["pA%d" % hf], writes=["ob0"])
                store(ob[0], "ob0", o_q, r0, D)
                for hf in range(2):
                    proj_block(pB[hf], 1024 + hf * 512, 512, "pB%d" % hf)
                    P.op("act", lambda e, hf=hf: e.activation(out=qf[:, hf * 512:(hf + 1) * 512],
                                                               in_=pB[hf][0:TT, :], func=AF.Sigmoid),
                         reads=["pB%d" % hf], writes=["qf"])
                P.op("dve", lambda e: e.tensor_tensor(out=qf[:], in0=qf[:], in1=oml[:], op=ALU.mult),
                     reads=["qf", "oml"], writes=["qf"])
                P.op("dve", lambda e: e.tensor_tensor(out=qf[:], in0=qf[:], in1=lb[:], op=ALU.add),
                     reads=["qf", "lb"], writes=["qf"])
                P.op("dve", lambda e: e.tensor_scalar(out=ob[1][:], in0=qf[:], scalar1=-1.0, scalar2=1.0,
                                                      op0=ALU.mult, op1=ALU.add),
                     reads=["qf"], writes=["ob1"])
                store(ob[1], "ob1", o_k, r0, D)
                P.op("act", lambda e: e.activation(out=of[1][:], in_=qf[:], func=AF.Ln),
                     reads=["qf"], writes=["of1"])
                store(of[1], "of1", o_lf, r0, D)
                for hf in range(2):
                    proj_block(pA[hf], 2048 + hf * 512, 512, "pA%d" % hf)
                    P.op("act", lambda e, hf=hf: e.copy(out=ob[2][:, hf * 512:(hf + 1) * 512], in_=pA[hf][0:TT, :]),
                         reads=["pA%d" % hf], writes=["ob2"])
                store(ob[2], "ob2", o_v, r0, D)
                for hf in range(2):
                    proj_block(pB[hf], 3072 + hf * 512, 512, "pB%d" % hf)
                    P.op("act", lambda e, hf=hf: e.activation(out=of[0][:, hf * 512:(hf + 1) * 512],
                                                               in_=pB[hf][0:TT, :], func=AF.Sigmoid),
                         reads=["pB%d" % hf], writes=["of0"])
                store(of[0], "of0", o_sg, r0, D)
        P.emit()
    return nc


GQ = 3 * TT
NG = L // GQ
SC = 2 * GQ
NSC = L // SC
KAUG = HD + 5


def attn_masks():
    m = np.zeros((TT, 3, GQ), np.float32)
    kk = np.arange(TT)[:, None]
    r = np.arange(GQ)[None, :]
    for j in range(3):
        m[:, j, :] = (kk + TT * j <= r)
    return m.astype(ml_dtypes.bfloat16)


def build_attn():
    nc, stack, P = new_prog()
    qT = dram_in(nc, "qT", [4, HD, L], BF16)
    kT = dram_in(nc, "kT", [4, HD, L], BF16)
    v = dram_in(nc, "v", [L, 4 * HD], BF16)
    lf = dram_in(nc, "lf", [4, L])
    msk = dram_in(nc, "msk", [TT, 3, GQ], BF16)
    o = dram_out(nc, "o", [L, 4 * HD])
    cq_scr = nc.dram_tensor("cq_scr", [2, 4, L], BF16, kind="Internal").ap()
    ck_scr = nc.dram_tensor("ck_scr", [3, 4, L], BF16, kind="Internal").ap()

    with stack:
        KA = P.sb("KA", [KAUG, 4, L], BF16)
        VA = P.sb("VA", [TT, NT_B, 4, HD + 1], BF16)
        M = P.sb("M", [TT, 3, GQ], BF16)
        QG = [P.sb("QG%d" % i, [KAUG, 4, GQ], BF16) for i in range(2)]
        PT = [P.sb("PT%d" % i, [TT, GQ], BF16) for i in range(3)]
        Sps = [P.ps("S%d" % i) for i in range(2)]
        Ops = [[P.ps("O%d_%d" % (b, i)) for i in range(3)] for b in range(2)]
        vst = [P.sb("vst%d" % i, [TT, 6, 4 * HD], BF16) for i in range(2)]
        ones = P.sb("ones", [4, SC], F32)
        lfc = [P.sb("lfc%d" % i, [4, SC], F32) for i in range(2)]
        cc = [P.sb("cc%d" % i, [4, SC], F32) for i in range(2)]
        ncc = P.sb("ncc", [4, SC], F32)
        r1 = P.sb("r1", [4, SC], F32)
        sq_hi = [P.sb("sqhi%d" % i, [4, SC], BF16) for i in range(2)]
        sq_lo = [P.sb("sqlo%d" % i, [4, SC], BF16) for i in range(2)]
        sk_hi = [P.sb("skhi%d" % i, [4, SC], BF16) for i in range(2)]
        sk_mi = [P.sb("skmi%d" % i, [4, SC], BF16) for i in range(2)]
        sk_lo = [P.sb("sklo%d" % i, [4, SC], BF16) for i in range(2)]
        rec = P.sb("rec", [TT, 4], F32)
        ob = [P.sb("ob%d" % i, [TT, 4 * HD], F32) for i in range(2)]

        P.op("sp", lambda e: e.dma_start(out=M[:], in_=msk[:, :, :]), writes=["M"], dma=True)
        P.op("pool", lambda e: e.memset(KA[64:KAUG, :, :], 1.0), writes=[("KAaug", "ms")])
        for i in range(2):
            P.op("pool", lambda e, i=i: e.memset(QG[i][64:KAUG, :, :], 1.0), writes=[("QGa", i)])
        P.op("pool", lambda e: e.memset(VA[:, :, :, HD:HD + 1], 1.0), writes=[("VA", "ones")])
        P.op("dve", lambda e: e.memset(ones[:], 1.0), writes=["ones"])
        for h in range(4):
            P.op("sp", lambda e, h=h: e.dma_start(out=KA[0:HD, h, :], in_=kT[h, :, :]),
                 writes=[("KA", h)], dma=True)
        for ci in range(NSC):
            b = ci % 2
            c0 = ci * SC
            P.op("sp", lambda e, b=b, c0=c0: e.dma_start(out=lfc[b][:], in_=lf[:, c0:c0 + SC]),
                 writes=["lfc%d" % b], dma=True)
            if ci == 0:
                P.op("dve", lambda e, b=b: e.tensor_tensor_scan(out=cc[b][:], data0=ones[:], data1=lfc[b][:],
                                                                 initial=0.0, op0=ALU.mult, op1=ALU.add),
                     reads=["ones", "lfc%d" % b], writes=["cc%d" % b])
            else:
                P.op("dve", lambda e, b=b: e.tensor_tensor_scan(out=cc[b][:], data0=ones[:], data1=lfc[b][:],
                                                                 initial=cc[1 - b][:, SC - 1:SC],
                                                                 op0=ALU.mult, op1=ALU.add),
                     reads=["ones", "lfc%d" % b, "cc%d" % (1 - b)], writes=["cc%d" % b])
            ck = "cc%d" % b
            P.op("dve", lambda e, b=b: e.tensor_copy(out=sq_hi[b][:], in_=cc[b][:]), reads=[ck], writes=["sqhi%d" % b])
            P.op("dve", lambda e, b=b: e.tensor_tensor(out=r1[:], in0=cc[b][:], in1=sq_hi[b][:], op=ALU.subtract),
                 reads=[ck, "sqhi%d" % b], writes=["r1"])
            P.op("dve", lambda e, b=b: e.tensor_copy(out=sq_lo[b][:], in_=r1[:]), reads=["r1"], writes=["sqlo%d" % b])
            P.op("dve", lambda e, b=b: e.tensor_scalar(out=ncc[:], in0=cc[b][:], scalar1=-1.0, scalar2=None, op0=ALU.mult),
                 reads=[ck], writes=["ncc"])
            P.op("dve", lambda e, b=b: e.tensor_copy(out=sk_hi[b][:], in_=ncc[:]), reads=["ncc"], writes=["skhi%d" % b])
            P.op("dve", lambda e, b=b: e.tensor_tensor(out=r1[:], in0=ncc[:], in1=sk_hi[b][:], op=ALU.subtract),
                 reads=["ncc", "skhi%d" % b], writes=["r1"])
            P.op("dve", lambda e, b=b: e.tensor_copy(out=sk_mi[b][:], in_=r1[:]), reads=["r1"], writes=["skmi%d" % b])
            P.op("dve", lambda e, b=b: e.tensor_tensor(out=ncc[:], in0=r1[:], in1=sk_mi[b][:], op=ALU.subtract),
                 reads=["r1", "skmi%d" % b], writes=["ncc"])
            P.op("dve", lambda e, b=b: e.tensor_copy(out=sk_lo[b][:], in_=ncc[:]), reads=["ncc"], writes=["sklo%d" % b])
            for j, (t_, k_) in enumerate(((sq_hi, "sqhi"), (sq_lo, "sqlo"))):
                P.op("sp", lambda e, t_=t_, j=j, b=b, c0=c0: e.dma_start(out=cq_scr[j, :, c0:c0 + SC], in_=t_[b][:]),
                     reads=["%s%d" % (k_, b)], writes=[("cqs", ci, j)], dma=True)
            for j, (t_, k_) in enumerate(((sk_hi, "skhi"), (sk_mi, "skmi"), (sk_lo, "sklo"))):
                P.op("sp", lambda e, t_=t_, j=j, b=b, c0=c0: e.dma_start(out=ck_scr[j, :, c0:c0 + SC], in_=t_[b][:]),
                     reads=["%s%d" % (k_, b)], writes=[("cks", ci, j)], dma=True)
            P.op("sp", lambda e, c0=c0: e.dma_start(out=KA[HD + 2:KAUG, :, c0:c0 + SC], in_=ck_scr[:, :, c0:c0 + SC]),
                 reads=[("cks", ci, 0), ("cks", ci, 1), ("cks", ci, 2), ("KAaug", "ms")],
                 writes=[("KAaug", ci)], dma=True)
        for s in range(NT_B // 6):
            b = s % 2
            P.op("sp", lambda e, s=s, b=b: e.dma_start(
                out=vst[b][:], in_=v[s * 6 * TT:(s + 1) * 6 * TT, :].rearrange("(t p) c -> p t c", p=TT)),
                writes=["vst%d" % b], dma=True)
            P.op("pool", lambda e, s=s, b=b: e.tensor_copy(
                out=VA[:, s * 6:(s + 1) * 6, :, 0:HD],
                in_=vst[b][:].rearrange("p t (h d) -> p t h d", d=HD)),
                reads=["vst%d" % b], writes=[("VA", s)])

        def load_q(g):
            b = g % 2
            P.op("sp", lambda e: e.dma_start(out=QG[b][0:HD, :, :],
                                             in_=qT[:, :, g * GQ:(g + 1) * GQ].rearrange("h d t -> d h t")),
                 writes=[("QGq", b)], dma=True)
            ci = g // 2
            P.op("sp", lambda e: e.dma_start(out=QG[b][HD:HD + 2, :, :], in_=cq_scr[:, :, g * GQ:(g + 1) * GQ]),
                 reads=[("cqs", ci, 0), ("cqs", ci, 1)], writes=[("QGa", b)], dma=True)

        units = [(g, h, kt) for g in range(NG) for h in range(4) for kt in range(3 * g + 3)]

        def qk(u):
            g, h, kt = units[u]
            S = Sps[u % 2]
            P.op("pe", lambda e: e.matmul(S[0:TT, 0:GQ], lhsT=KA[:, h, kt * TT:(kt + 1) * TT],
                                          rhs=QG[g % 2][:, h, :], start=True, stop=True),
                 reads=[("KA", h), ("KAaug", "ms"), ("KAaug", kt // 6), ("QGq", g % 2), ("QGa", g % 2)],
                 writes=["S%d" % (u % 2)])

        def rest(u):
            g, h, kt = units[u]
            S = Sps[u % 2]
            pt = PT[u % 3]
            pk = "PT%d" % (u % 3)
            P.op("act", lambda e: e.activation(out=pt[:], in_=S[0:TT, 0:GQ], func=AF.Exp),
                 reads=["S%d" % (u % 2)], writes=[pk])
            if kt >= 3 * g:
                m = kt - 3 * g
                P.op("dve", lambda e: e.tensor_tensor(out=pt[:], in0=pt[:], in1=M[:, m, :], op=ALU.mult),
                     reads=[pk, "M"], writes=[pk])
            for i in range(3):
                if kt <= 3 * g + i:
                    Ob = Ops[g % 2][i]
                    P.op("pe", lambda e, i=i, Ob=Ob: e.matmul(
                        Ob[0:TT, h * (HD + 1):(h + 1) * (HD + 1)], lhsT=pt[:, i * TT:(i + 1) * TT],
                        rhs=VA[:, kt, h, :], start=(kt == 0), stop=(kt == 3 * g + i)),
                        reads=[pk, ("VA", kt // 6), ("VA", "ones")], writes=["O%d_%d" % (g % 2, i)])

        def finish(g):
            for i in range(3):
                Ob = Ops[g % 2][i]
                okey = "O%d_%d" % (g % 2, i)
                o3 = Ob[0:TT, 0:4 * (HD + 1)].rearrange("p (h d) -> p h d", d=HD + 1)
                obuf = ob[i % 2]
                P.op("dve", lambda e, o3=o3: e.reciprocal(out=rec[:].unsqueeze(2), in_=o3[:, :, HD:HD + 1]),
                     reads=[okey], writes=["rec"])
                P.op("dve", lambda e, o3=o3, obuf=obuf: e.tensor_tensor(
                    out=obuf[:].rearrange("p (h d) -> p h d", d=HD), in0=o3[:, :, 0:HD],
                    in1=rec[:].unsqueeze(2).to_broadcast([TT, 4, HD]), op=ALU.mult),
                    reads=[okey, "rec"], writes=["ob%d" % (i % 2)])
                r0 = (3 * g + i) * TT
                P.op("pool", lambda e, obuf=obuf, r0=r0: e.dma_start(out=o[r0:r0 + TT, :], in_=obuf[:]),
                     reads=["ob%d" % (i % 2)], dma=True, is_out=True)

        load_q(0)
        load_q(1)
        qk(0)
        for u in range(len(units)):
            g, h, kt = units[u]
            if u + 1 < len(units):
                qk(u + 1)
            rest(u)
            if h == 3 and kt == 3 * g + 2:
                finish(g)
                if g + 2 < NG:
                    load_q(g + 2)
        P.emit()
    return nc


def build_mixout():
    nc, stack, P = new_prog()
    oT = dram_in(nc, "oT", [8, 128, CH])
    sgT = dram_in(nc, "sgT", [8, 128, CH])
    hres = dram_in(nc, "hres", [CH, D])
    w_out = dram_in(nc, "w_out", [D, D])
    fnorm = dram_in(nc, "fnorm", [1, D])
    o_h = dram_out(nc, "o_h", [CH, D])
    o_xn = dram_out(nc, "o_xn", [CH, D], BF16)
    with stack:
        g_bc = bcast_load(P, fnorm[0, :], D, "g_bc", TT)
        wo = load_weight_bf16(P, w_out, D, D, "wo", col_block=1024)
        NB = 2
        ot = [P.sb("ot%d" % i, [128, 8, TT], F32) for i in range(NB)]
        st = [P.sb("st%d" % i, [128, 8, TT], F32) for i in range(NB)]
        ht = [P.sb("ht%d" % i, [TT, D], F32) for i in range(NB)]
        og = P.sb("og", [128, 8, TT], BF16)
        hm = [P.sb("hm%d" % i, [TT, D], F32) for i in range(NB)]
        xn = [P.sb("xn%d" % i, [TT, D], BF16) for i in range(NB)]
        junk = P.sb("junk", [TT, D], F32)
        ss = P.sb("ss", [TT, 1], F32)
        rstd = P.sb("rstd", [TT, 1], F32)
        pW = [P.ps("pW%d" % i) for i in range(4)]
        for t in range(NT_C):
            b = t % NB
            c0 = t * TT
            P.op("sp", lambda e, b=b, c0=c0: e.dma_start(out=ot[b][:], in_=oT[:, :, c0:c0 + TT].rearrange("c p t -> p c t")),
                 writes=["ot%d" % b], dma=True)
            P.op("sp", lambda e, b=b, c0=c0: e.dma_start(out=st[b][:], in_=sgT[:, :, c0:c0 + TT].rearrange("c p t -> p c t")),
                 writes=["st%d" % b], dma=True)
            P.op("sp", lambda e, b=b, c0=c0: e.dma_start(out=ht[b][:], in_=hres[c0:c0 + TT, :]),
                 writes=["ht%d" % b], dma=True)
            P.op("dve", lambda e, b=b: e.tensor_tensor(out=og[:], in0=ot[b][:], in1=st[b][:], op=ALU.mult),
                 reads=["ot%d" % b, "st%d" % b], writes=["og"])
            for hf in range(2):
                pw = pW[(t % 2) * 2 + hf]
                pk = "pW%d" % ((t % 2) * 2 + hf)
                for dc in range(8):
                    P.op("pe", lambda e, dc=dc, pw=pw, hf=hf: e.matmul(pw[0:TT, :], lhsT=og[:, dc, :],
                                                                        rhs=wo[:, dc, hf * 512:(hf + 1) * 512],
                                                                        start=(dc == 0), stop=(dc == 7)),
                         reads=["og"] + P.wkeys["wo"], writes=[pk])
                P.op("dve", lambda e, b=b, pw=pw, hf=hf: e.tensor_tensor(out=hm[b][:, hf * 512:(hf + 1) * 512], in0=pw[0:TT, :],
                                                                          in1=ht[b][:, hf * 512:(hf + 1) * 512], op=ALU.add),
                     reads=[pk, "ht%d" % b], writes=["hm%d" % b])
            P.op("pool", lambda e, b=b, c0=c0: e.dma_start(out=o_h[c0:c0 + TT, :], in_=hm[b][:]),
                 reads=["hm%d" % b], dma=True, is_out=True)
            P.op("act", lambda e, b=b: e.activation(out=junk[:], in_=hm[b][:], func=AF.Square, accum_out=ss[:, 0:1]),
                 reads=["hm%d" % b], writes=["junk", "ss"])
            rms_rstd(P, ss, rstd, "ss", "rstd", TT, 1.0 / D)
            P.op("dve", lambda e, b=b: e.scalar_tensor_tensor(out=xn[b][:], in0=hm[b][:], scalar=rstd[:, 0:1],
                                                               in1=g_bc[:], op0=ALU.mult, op1=ALU.mult),
                 reads=["hm%d" % b, "rstd", "g_bc"], writes=["xn%d" % b])
            P.op("pool", lambda e, b=b, c0=c0: e.dma_start(out=o_xn[c0:c0 + TT, :], in_=xn[b][:]),
                 reads=["xn%d" % b], dma=True, is_out=True)
        P.emit()
    return nc


FG = 2 * TT
NFG = CH // FG


def build_ffn():
    nc, stack, P = new_prog()
    xT = dram_in(nc, "xT", [8, 128, CH + 2], BF16)
    hmid = dram_in(nc, "hmid", [CH, D])
    w_gate = dram_in(nc, "w_gate", [D, FF])
    w_up = dram_in(nc, "w_up", [D, FF])
    w_down = dram_in(nc, "w_down", [FF, D])
    cwT = dram_in(nc, "cwT", [128, NFC, 3])
    cb = dram_in(nc, "cb", [128, NFC, 1])
    o_h = dram_out(nc, "o_h", [CH, D])
    with stack:
        cw_sb = P.sb("cw_sb", [128, NFC, 3], F32)
        cb_sb = P.sb("cb_sb", [128, NFC, 1], F32)
        P.op("sp", lambda e: e.dma_start(out=cw_sb[:], in_=cwT[:, :, :]),
             writes=["cw"], dma=True)
        P.op("sp", lambda e: e.dma_start(out=cb_sb[:], in_=cb[:, :, :]),
             writes=["cb"], dma=True)
        wg = load_weight_bf16(P, w_gate, D, FF, "wg", col_block=1408)
        wu = load_weight_bf16(P, w_up, D, FF, "wu", col_block=1408)
        wd = load_weight_bf16(P, w_down, FF, D, "wd", col_block=1024)
        xs = [P.sb("xs%d" % i, [128, 8, FG + 2], BF16) for i in range(2)]
        hT = [P.sb("hT%d" % i, [128, NFC, FG], BF16) for i in range(2)]
        yb = [P.sb("yb%d" % i, [128, FG], F32) for i in range(2)]
        sb_ = [P.sb("sl%d" % i, [128, FG], F32) for i in range(2)]
        hmt = [P.sb("hmt%d" % i, [TT, D], F32) for i in range(2)]
        outt = [P.sb("outt%d" % i, [TT, D], F32) for i in range(2)]
        pa = [P.ps("pa%d" % i) for i in range(2)]
        pu = [P.ps("pu%d" % i) for i in range(2)]
        pd = [P.ps("pd%d" % i) for i in range(4)]
        it = 0
        for g in range(NFG):
            b = g % 2
            c0 = g * FG
            P.op("sp", lambda e, b=b, c0=c0: e.dma_start(out=xs[b][:], in_=xT[:, :, c0:c0 + FG + 2].rearrange("c p t -> p c t")),
                 writes=["xs%d" % b], dma=True)
            for fc in range(NFC):
                j = it % 2
                it += 1
                for dc in range(8):
                    P.op("pe", lambda e, dc=dc, j=j, fc=fc, b=b: e.matmul(pa[j][:, 0:FG + 2], lhsT=wg[:, dc, fc * 128:(fc + 1) * 128],
                                                                           rhs=xs[b][:, dc, :], start=(dc == 0), stop=(dc == 7)),
                         reads=["xs%d" % b] + P.wkeys["wg"], writes=["pa%d" % j])
                for dc in range(8):
                    P.op("pe", lambda e, dc=dc, j=j, fc=fc, b=b: e.matmul(pu[j][:, 0:FG], lhsT=wu[:, dc, fc * 128:(fc + 1) * 128],
                                                                           rhs=xs[b][:, dc, 2:FG + 2], start=(dc == 0), stop=(dc == 7)),
                         reads=["xs%d" % b] + P.wkeys["wu"], writes=["pu%d" % j])
                P.op("act", lambda e, j=j, fc=fc: e.activation(out=yb[j][:], in_=pa[j][:, 2:FG + 2], func=AF.Identity,
                                                                scale=cw_sb[:, fc, 2:3], bias=cb_sb[:, fc, 0:1]),
                     reads=["pa%d" % j, "cw", "cb"], writes=["yb%d" % j])
                P.op("dve", lambda e, j=j, fc=fc: e.scalar_tensor_tensor(out=yb[j][:], in0=pa[j][:, 1:FG + 1], scalar=cw_sb[:, fc, 1:2],
                                                                          in1=yb[j][:], op0=ALU.mult, op1=ALU.add),
                     reads=["pa%d" % j, "cw", "yb%d" % j], writes=["yb%d" % j])
                P.op("dve", lambda e, j=j, fc=fc: e.scalar_tensor_tensor(out=yb[j][:], in0=pa[j][:, 0:FG], scalar=cw_sb[:, fc, 0:1],
                                                                          in1=yb[j][:], op0=ALU.mult, op1=ALU.add),
                     reads=["pa%d" % j, "cw", "yb%d" % j], writes=["yb%d" % j])
                P.op("act", lambda e, j=j: e.activation(out=sb_[j][:], in_=yb[j][:], func=AF.Silu),
                     reads=["yb%d" % j], writes=["sl%d" % j])
                P.op("dve", lambda e, j=j, fc=fc, b=b: e.tensor_tensor(out=hT[b][:, fc, :], in0=sb_[j][:], in1=pu[j][:, 0:FG], op=ALU.mult),
                     reads=["sl%d" % j, "pu%d" % j], writes=[("hT", b, fc)])
            for i in range(2):
                r0 = c0 + i * TT
                P.op("sp", lambda e, i=i, r0=r0: e.dma_start(out=hmt[i][:], in_=hmid[r0:r0 + TT, :]),
                     writes=["hmt%d" % i], dma=True)
                for hf in range(2):
                    pdt = pd[i * 2 + hf]
                    pk = "pd%d" % (i * 2 + hf)
                    for fc in range(NFC):
                        P.op("pe", lambda e, fc=fc, pdt=pdt, i=i, hf=hf, b=b: e.matmul(
                            pdt[0:TT, :], lhsT=hT[b][:, fc, i * TT:(i + 1) * TT], rhs=wd[:, fc, hf * 512:(hf + 1) * 512],
                            start=(fc == 0), stop=(fc == NFC - 1)),
                            reads=[("hT", b, fc)] + P.wkeys["wd"], writes=[pk])
                    P.op("dve", lambda e, pdt=pdt, i=i, hf=hf: e.tensor_tensor(out=outt[i][:, hf * 512:(hf + 1) * 512], in0=pdt[0:TT, :],
                                                                                in1=hmt[i][:, hf * 512:(hf + 1) * 512], op=ALU.add),
                         reads=[pk, "hmt%d" % i], writes=["outt%d" % i])
                P.op("pool", lambda e, i=i, r0=r0: e.dma_start(out=o_h[r0:r0 + TT, :], in_=outt[i][:]),
                     reads=["outt%d" % i], dma=True, is_out=True)
        P.emit()
    return nc


HC = 57
NHC = L // HC
SPC = 18
SPT = SPC * HC
NSP = NHC // SPC
HMID = 28


def hgrn_mask():
    s = np.arange(HC)[:, None]
    t = np.arange(HC)[None, :]
    return (s <= t).astype(np.float32)


def build_hgrn():
    nc, stack, P = new_prog()
    qT = dram_in(nc, "qT", [2, DK, L], BF16)
    kT = dram_in(nc, "kT", [2, DK, L], BF16)
    lfT = dram_in(nc, "lfT", [2, DK, L])
    v = dram_in(nc, "v", [L, 2 * DK], BF16)
    o_gain = dram_in(nc, "o_gain", [1, DK])
    msk = dram_in(nc, "msk", [HC, HC])
    on = dram_out(nc, "on", [L, 2 * DK])
    with stack:
        ident = make_identity(P)
        gain_bc = bcast_load(P, o_gain[0, :], DK, "gain_bc", HC)
        M = P.sb("M", [HC, HC], F32)
        P.op("sp", lambda e: e.dma_start(out=M[:], in_=msk[:, :]), writes=["M"], dma=True)
        ones = P.sb("ones", [128, HC], F32)
        P.op("pool", lambda e: e.memset(ones[:], 1.0), writes=["ones"])
        qs = [P.sb("qs%d" % i, [128, 2, SPT], BF16) for i in range(2)]
        ks = [P.sb("ks%d" % i, [128, 2, SPT], BF16) for i in range(2)]
        lfs = [P.sb("lfs%d" % i, [128, 2, SPT], F32) for i in range(2)]
        vs = [P.sb("vs%d" % i, [HC, SPC, 2 * DK], BF16) for i in range(2)]
        S = [P.sb("S%d" % h, [128, DK], F32) for h in range(2)]
        Sb = [P.sb("Sb%d" % h, [128, DK], BF16) for h in range(2)]
        bT = [P.sb("bT%d" % h, [128, HC], F32) for h in range(2)]
        nmid = [P.sb("nmid%d" % h, [128, 1], F32) for h in range(2)]
        emid = [P.sb("emid%d" % h, [128, 1], F32) for h in range(2)]
        eend = [P.sb("eend%d" % h, [128, 1], F32) for h in range(2)]
        E1 = [P.sb("E1_%d" % h, [128, HC], F32) for h in range(2)]
        E2 = [P.sb("E2_%d" % h, [128, HC], F32) for h in range(2)]
        E3 = [P.sb("E3_%d" % h, [128, HC], F32) for h in range(2)]
        qt = [P.sb("qt%d" % h, [128, HC], BF16) for h in range(2)]
        kt = [P.sb("kt%d" % h, [128, HC], BF16) for h in range(2)]
        kh = [P.sb("kh%d" % h, [128, HC], BF16) for h in range(2)]
        sc = [P.sb("sc%d" % h, [HC, HC], BF16) for h in range(2)]
        khT = [P.sb("khT%d" % h, [HC, DK], BF16) for h in range(2)]
        junk = [P.sb("junk%d" % h, [HC, DK], F32) for h in range(2)]
        ss = [P.sb("ss%d" % h, [HC, 1], F32) for h in range(2)]
        rstd = [P.sb("rstd%d" % h, [HC, 1], F32) for h in range(2)]
        obuf = [P.sb("obuf%d" % i, [HC, 2 * DK], F32) for i in range(2)]
        pS = [P.ps("pS%d" % h) for h in range(2)]
        pO = [P.ps("pO%d" % h) for h in range(2)]
        pK = [P.ps("pK%d" % h, BF16) for h in range(2)]
        pD = [P.ps("pD%d" % h) for h in range(2)]
        for h in range(2):
            P.op("pool", lambda e, h=h: e.memset(S[h][:], 0.0), writes=["S%d" % h])

        def load_sp(sp):
            b = sp % 2
            c0 = sp * SPT
            P.op("sp", lambda e: e.dma_start(out=qs[b][:], in_=qT[:, :, c0:c0 + SPT].rearrange("h d t -> d h t")),
                 writes=["qs%d" % b], dma=True)
            P.op("sp", lambda e: e.dma_start(out=ks[b][:], in_=kT[:, :, c0:c0 + SPT].rearrange("h d t -> d h t")),
                 writes=["ks%d" % b], dma=True)
            P.op("sp", lambda e: e.dma_start(out=lfs[b][:], in_=lfT[:, :, c0:c0 + SPT].rearrange("h d t -> d h t")),
                 writes=["lfs%d" % b], dma=True)
            P.op("sp", lambda e: e.dma_start(out=vs[b][:], in_=v[c0:c0 + SPT, :].rearrange("(c p) d -> p c d", p=HC)),
                 writes=["vs%d" % b], dma=True)

        def do_chunk(ch):
            sp = ch // SPC
            b = sp % 2
            cl = ch % SPC
            lc = cl * HC
            if cl == 0 and sp + 1 < NSP:
                load_sp(sp + 1)
            par = ch % 2
            for h in range(2):
                H = str(h)
                P.op("dve", lambda e, h=h: e.tensor_tensor_scan(out=bT[h][:], data0=ones[:], data1=lfs[b][:, h, lc:lc + HC],
                                                                 initial=0.0, op0=ALU.mult, op1=ALU.add),
                     reads=["ones", "lfs%d" % b], writes=["bT" + H])
                P.op("dve", lambda e, h=h: e.tensor_scalar(out=nmid[h][:], in0=bT[h][:, HMID:HMID + 1], scalar1=-1.0, scalar2=None,
                                                           op0=ALU.mult),
                     reads=["bT" + H], writes=["nmid" + H])
                P.op("act", lambda e, h=h: e.activation(out=E1[h][:], in_=bT[h][:], func=AF.Exp, bias=nmid[h][:, 0:1]),
                     reads=["bT" + H, "nmid" + H], writes=["E1_" + H])
                P.op("act", lambda e, h=h: e.activation(out=E2[h][:], in_=bT[h][:], func=AF.Exp, scale=-1.0,
                                                        bias=bT[h][:, HMID:HMID + 1]),
                     reads=["bT" + H], writes=["E2_" + H])
                P.op("act", lambda e, h=h: e.activation(out=E3[h][:], in_=bT[h][:], func=AF.Exp, scale=-1.0,
                                                        bias=bT[h][:, HC - 1:HC]),
                     reads=["bT" + H], writes=["E3_" + H])
                P.op("act", lambda e, h=h: e.activation(out=emid[h][:], in_=bT[h][:, HMID:HMID + 1], func=AF.Exp),
                     reads=["bT" + H], writes=["emid" + H])
                P.op("act", lambda e, h=h: e.activation(out=eend[h][:], in_=bT[h][:, HC - 1:HC], func=AF.Exp),
                     reads=["bT" + H], writes=["eend" + H])
                P.op("dve", lambda e, h=h: e.tensor_tensor(out=qt[h][:], in0=qs[b][:, h, lc:lc + HC], in1=E1[h][:], op=ALU.mult),
                     reads=["qs%d" % b, "E1_" + H], writes=["qt" + H])
                P.op("pool", lambda e, h=h: e.tensor_tensor(out=kt[h][:], in0=ks[b][:, h, lc:lc + HC], in1=E2[h][:], op=ALU.mult),
                     reads=["ks%d" % b, "E2_" + H], writes=["kt" + H])
                P.op("pool", lambda e, h=h: e.tensor_tensor(out=kh[h][:], in0=ks[b][:, h, lc:lc + HC], in1=E3[h][:], op=ALU.mult),
                     reads=["ks%d" % b, "E3_" + H], writes=["kh" + H])
                P.op("dve", lambda e, h=h: e.tensor_scalar(out=Sb[h][:], in0=S[h][:], scalar1=emid[h][:, 0:1], scalar2=None,
                                                           op0=ALU.mult),
                     reads=["S" + H, "emid" + H], writes=["Sb" + H])
                P.op("pe", lambda e, h=h: e.matmul(pS[h][0:HC, 0:HC], lhsT=kt[h][:], rhs=qt[h][:], start=True, stop=True),
                     reads=["kt" + H, "qt" + H], writes=["pS" + H])
                P.op("dve", lambda e, h=h: e.tensor_tensor(out=sc[h][:], in0=pS[h][0:HC, 0:HC], in1=M[:], op=ALU.mult),
                     reads=["pS" + H, "M"], writes=["sc" + H])
                P.op("pe", lambda e, h=h: e.matmul(pO[h][0:HC, 0:DK], lhsT=sc[h][:], rhs=vs[b][:, cl, h * DK:(h + 1) * DK],
                                                   start=True, stop=False),
                     reads=["sc" + H, "vs%d" % b], writes=["pO" + H])
                P.op("pe", lambda e, h=h: e.matmul(pO[h][0:HC, 0:DK], lhsT=qt[h][:], rhs=Sb[h][:], start=False, stop=True),
                     reads=["qt" + H, "Sb" + H], writes=["pO" + H])
                P.op("act", lambda e, h=h: e.activation(out=junk[h][:], in_=pO[h][0:HC, 0:DK], func=AF.Square,
                                                        accum_out=ss[h][:, 0:1]),
                     reads=["pO" + H], writes=["junk" + H, "ss" + H])
                rms_rstd(P, ss[h], rstd[h], "ss" + H, "rstd" + H, HC, 1.0 / DK)
                P.op("dve", lambda e, h=h: e.scalar_tensor_tensor(out=obuf[par][:, h * DK:(h + 1) * DK], in0=pO[h][0:HC, 0:DK],
                                                                  scalar=rstd[h][:, 0:1], in1=gain_bc[:],
                                                                  op0=ALU.mult, op1=ALU.mult),
                     reads=["pO" + H, "rstd" + H, "gain_bc"], writes=[("obuf", par, h)])
                P.op("pe", lambda e, h=h: e.transpose(out=pK[h][0:HC, 0:DK], in_=kh[h][:], identity=ident[:, :]),
                     reads=["kh" + H, "ident"], writes=["pK" + H])
                P.op("act", lambda e, h=h: e.copy(out=khT[h][:], in_=pK[h][0:HC, 0:DK]),
                     reads=["pK" + H], writes=["khT" + H])
                P.op("pe", lambda e, h=h: e.matmul(pD[h][:, 0:DK], lhsT=khT[h][:], rhs=vs[b][:, cl, h * DK:(h + 1) * DK],
                                                   start=True, stop=True),
                     reads=["khT" + H, "vs%d" % b], writes=["pD" + H])
                P.op("dve", lambda e, h=h: e.scalar_tensor_tensor(out=S[h][:], in0=S[h][:], scalar=eend[h][:, 0:1],
                                                                  in1=pD[h][:, 0:DK], op0=ALU.mult, op1=ALU.add),
                     reads=["S" + H, "eend" + H, "pD" + H], writes=["S" + H])
            r0 = ch * HC
            P.op("pool", lambda e, par=par, r0=r0: e.dma_start(out=on[r0:r0 + HC, :], in_=obuf[par][:]),
                 reads=[("obuf", par, 0), ("obuf", par, 1)], dma=True, is_out=True)

        load_sp(0)
        for ch in range(NHC):
            do_chunk(ch)
        P.emit()
    return nc


def _run(nc, in_maps):
    res = run_bass_kernel_spmd(nc, in_maps, core_ids=list(range(NCORES)))
    return res.results


def _featT(a):
    return np.ascontiguousarray(a.T).reshape(8, 128, a.shape[0])


def _mix_ffn(o_flat, sg_flat, h_flat, w_out, fnorm, w_gate, w_up, conv_w, conv_b, w_down):
    in_maps = []
    for r in range(NCORES):
        sl = slice(r * CH, (r + 1) * CH)
        in_maps.append({"oT": _featT(o_flat[sl]), "sgT": _featT(sg_flat[sl]),
                        "hres": np.ascontiguousarray(h_flat[sl]), "w_out": w_out, "fnorm": fnorm})
    res = _run(build_mixout(), in_maps)
    hmid = np.concatenate([r["o_h"] for r in res], axis=0)
    xn2 = np.concatenate([r["o_xn"] for r in res], axis=0)
    cwT = np.ascontiguousarray(conv_w.T.reshape(NFC, 128, 3).transpose(1, 0, 2))
    cb = np.ascontiguousarray(conv_b.reshape(NFC, 128).T).reshape(128, NFC, 1)
    in_maps = []
    for r in range(NCORES):
        xh = np.zeros((CH + 2, D), xn2.dtype)
        if r % 4 == 0:
            xh[2:] = xn2[r * CH:(r + 1) * CH]
        else:
            xh[:] = xn2[r * CH - 2:(r + 1) * CH]
        in_maps.append({"xT": _featT(xh), "hmid": np.ascontiguousarray(hmid[r * CH:(r + 1) * CH]),
                        "w_gate": w_gate, "w_up": w_up, "w_down": w_down, "cwT": cwT, "cb": cb})
    res = _run(build_ffn(), in_maps)
    return np.concatenate([r["o_h"] for r in res], axis=0)


def kernel(x, meta_tokens, fox_norm, fox_w_in, fox_b_f, fox_q_gain, fox_k_gain, fox_w_out,
           hgrn_norm, hgrn_w_in, hgrn_lower_bounds, hgrn_o_gain, hgrn_w_out,
           ffn_norm, ffn_w_gate, ffn_w_up, ffn_conv_w, ffn_conv_b, ffn_w_down):
    f32 = lambda a: np.ascontiguousarray(np.asarray(a, dtype=np.float32))
    x = f32(x)
    meta = np.broadcast_to(f32(meta_tokens)[None], (B, NMETA, D))
    h0 = np.ascontiguousarray(np.concatenate([meta, x], axis=1)).reshape(B * L, D)

    in_maps = [{"x": np.ascontiguousarray(h0[r * CH:(r + 1) * CH]), "gnorm": f32(fox_norm[0:1]),
                "w_in": f32(fox_w_in[0]), "b_f": f32(fox_b_f[0:1]), "q_gain": f32(fox_q_gain[0:1]),
                "k_gain": f32(fox_k_gain[0:1])} for r in range(NCORES)]
    res = _run(build_proj("fox"), in_maps)
    cat = lambda key: np.concatenate([r[key] for r in res], axis=0)
    q = cat("o_q").reshape(B, L, H_FOX, HD)
    k = cat("o_k").reshape(B, L, H_FOX, HD)
    v = cat("o_v").reshape(B, L, H_FOX * HD)
    lf = cat("o_lf").reshape(B, L, H_FOX)
    sg = cat("o_sg")
    msk = attn_masks()
    in_maps = []
    for r in range(NCORES):
        b, hg = r // 4, r % 4
        hs = slice(4 * hg, 4 * hg + 4)
        in_maps.append({"qT": np.ascontiguousarray(q[b, :, hs, :].transpose(1, 2, 0)),
                        "kT": np.ascontiguousarray(k[b, :, hs, :].transpose(1, 2, 0)),
                        "v": np.ascontiguousarray(v[b, :, 256 * hg:256 * (hg + 1)]),
                        "lf": np.ascontiguousarray(lf[b, :, hs].T), "msk": msk})
    res = _run(build_attn(), in_maps)
    o = np.zeros((B, L, D), np.float32)
    for r in range(NCORES):
        b, hg = r // 4, r % 4
        o[b, :, 256 * hg:256 * (hg + 1)] = res[r]["o"]
    h1 = _mix_ffn(o.reshape(B * L, D), sg, h0, f32(fox_w_out[0]), f32(ffn_norm[0:1]), f32(ffn_w_gate[0]),
                  f32(ffn_w_up[0]), f32(ffn_conv_w[0]), f32(ffn_conv_b[0]), f32(ffn_w_down[0]))

    in_maps = [{"x": np.ascontiguousarray(h1[r * CH:(r + 1) * CH]), "gnorm": f32(hgrn_norm[0:1]),
                "w_in": f32(hgrn_w_in[0]), "lbs": f32(hgrn_lower_bounds)} for r in range(NCORES)]
    res = _run(build_proj("hgrn"), in_maps)
    q = cat("o_q").reshape(B, L, H_HG, DK)
    k = cat("o_k").reshape(B, L, H_HG, DK)
    lf = cat("o_lf").reshape(B, L, H_HG, DK)
    v = cat("o_v").reshape(B, L, H_HG * DK)
    sg = cat("o_sg")
    hm = hgrn_mask()
    in_maps = []
    for r in range(NCORES):
        b, hp = r // 4, r % 4
        hs = slice(2 * hp, 2 * hp + 2)
        in_maps.append({"qT": np.ascontiguousarray(q[b, :, hs, :].transpose(1, 2, 0)),
                        "kT": np.ascontiguousarray(k[b, :, hs, :].transpose(1, 2, 0)),
                        "lfT": np.ascontiguousarray(lf[b, :, hs, :].transpose(1, 2, 0)),
                        "v": np.ascontiguousarray(v[b, :, 256 * hp:256 * (hp + 1)]),
                        "o_gain": f32(hgrn_o_gain[0:1]), "msk": hm})
    res = _run(build_hgrn(), in_maps)
    o = np.zeros((B, L, D), np.float32)
    for r in range(NCORES):
        b, hp = r // 4, r % 4
        o[b, :, 256 * hp:256 * (hp + 1)] = res[r]["on"]
    h2 = _mix_ffn(o.reshape(B * L, D), sg, h1, f32(hgrn_w_out[0]), f32(ffn_norm[1:2]), f32(ffn_w_gate[1]),
                  f32(ffn_w_up[1]), f32(ffn_conv_w[1]), f32(ffn_conv_b[1]), f32(ffn_w_down[1]))
    return np.ascontiguousarray(h2.reshape(B, L, D)[:, NMETA:, :]).astype(np.float32)
```
